# Optimizing a Trainium2 kernel written in Bass

```python
import math
import jax
import jax.numpy as jnp
from jax import lax
import numpy as np


D_MODEL = 2048
BATCH = 2
SEQ = 16384
DEPTH = 4

GRID_W = 64
CTX_LEN = 256
N_MIXERS = 3
N_HYENA = len(range(0, DEPTH, N_MIXERS))
N_MLA = len(range(1, DEPTH, N_MIXERS))
N_POOL = len(range(2, DEPTH, N_MIXERS))

D_FF = -(-(8 * D_MODEL) // (3 * 256)) * 256

ALPHA = (2.0 * DEPTH) ** 0.25
BETA = (8.0 * DEPTH) ** -0.25
LN_EPS = 1e-6
RMS_EPS = 1e-6

HY_EMB = 33
HY_BANDS = (HY_EMB - 1) // 2
HY_FILT = 64
HY_TARGET = 1e-2
HY_FAST = 0.3
HY_SLOW = 1.5
HY_MIN_DECAY = math.log(HY_TARGET) / HY_SLOW
HY_MAX_DECAY = math.log(HY_TARGET) / HY_FAST

MLA_HEADS = D_MODEL // 128
QK_NOPE = 128
QK_ROPE = 64
V_HEAD = 128
Q_RANK = D_MODEL // 4
KV_RANK = D_MODEL // 4
ROPE_PAIRS = QK_ROPE // 4
ROPE_THETA = 10000.0
Q_BLOCK = 128
ATTN_SCALE = (QK_NOPE + QK_ROPE) ** -0.5

POOL_WINDOWS = (2, 4, 8, 16)
POOL_GROUPS = len(POOL_WINDOWS)
POOL_CH = D_MODEL // POOL_GROUPS

kernel_name = 'hybrid_dit_hyena_mla_pool'


def layer_norm(x, g, b):
    xf = x.astype(jnp.float32)
    mu = jnp.mean(xf, axis=-1, keepdims=True)
    var = jnp.mean(jnp.square(xf - mu), axis=-1, keepdims=True)
    return ((xf - mu) * lax.rsqrt(var + LN_EPS) * g + b).astype(x.dtype)


def rms_norm(x, g):
    xf = x.astype(jnp.float32)
    return (xf * lax.rsqrt(jnp.mean(jnp.square(xf), axis=-1, keepdims=True) + RMS_EPS) * g).astype(x.dtype)


def modulate(x, shift, scale):
    return x * (1 + scale) + shift


def post_norm(x, y, g, b):
    return layer_norm(ALPHA * x + y, g, b)


def swiglu(u, w_gate, w_up, w_down):
    return (jax.nn.silu(u @ w_gate) * (u @ w_up)) @ w_down


def axial_rope_tables(L):
    rows = L // GRID_W
    row = jnp.repeat(jnp.arange(rows, dtype=jnp.float32), GRID_W)
    col = jnp.tile(jnp.arange(GRID_W, dtype=jnp.float32), rows)
    inv = ROPE_THETA ** (-jnp.arange(ROPE_PAIRS, dtype=jnp.float32) / ROPE_PAIRS)
    ang = jnp.stack([row[:, None] * inv, col[:, None] * inv], axis=1)
    ang = jnp.broadcast_to(ang[:, :, None, :], (L, 2, 2, ROPE_PAIRS)).reshape(L, QK_ROPE)
    return jnp.cos(ang), jnp.sin(ang)


def apply_axial_rope(x, cos, sin):
    xs = x.reshape(x.shape[:-1] + (2, 2, ROPE_PAIRS))
    rot = jnp.stack([-xs[..., 1, :], xs[..., 0, :]], axis=-2).reshape(x.shape)
    return (x * cos + rot * sin).astype(x.dtype)


def short_conv3(u, w, b):
    L = u.shape[1]
    up = jnp.pad(u, ((0, 0), (1, 1), (0, 0)))
    return up[:, :L] * w[0] + up[:, 1:L + 1] * w[1] + up[:, 2:] * w[2] + b


def implicit_filter(L, f_w_in, f_w_hid, f_b, f_freq, f_w_out):
    f32 = jnp.float32
    pos = jnp.arange(L, dtype=f32)
    t = pos / (L - 1)
    bands = jnp.linspace(1e-4, HY_BANDS - 1, HY_BANDS, dtype=f32)
    ang = (2.0 * math.pi / L) * pos[:, None] * bands[None, :]
    z = jnp.concatenate([t[:, None], jnp.cos(ang), -jnp.sin(ang)], axis=-1)
    f_w_in, f_w_hid, f_b, f_freq, f_w_out = [a.astype(f32) for a in (f_w_in, f_w_hid, f_b, f_freq, f_w_out)]
    g = jnp.sin(f_freq[0] * (z @ f_w_in + f_b[0]))
    g = jnp.sin(f_freq[1] * (g @ f_w_hid[0] + f_b[1]))
    g = jnp.sin(f_freq[2] * (g @ f_w_hid[1] + f_b[2]))
    filt = g @ f_w_out
    dist = jnp.abs(pos - L // 2) / (L // 2)
    deltas = jnp.abs(jnp.linspace(HY_MIN_DECAY, HY_MAX_DECAY, filt.shape[-1], dtype=f32))
    filt = filt * jnp.exp(-dist[:, None] * deltas[None, :])
    return filt / jnp.sum(jnp.abs(filt), axis=0, keepdims=True)


def centred_long_conv(u, h):
    L = u.shape[1]
    n = 2 * L
    U = jnp.fft.rfft(u.astype(jnp.float32), n=n, axis=1)
    H = jnp.fft.rfft(h.astype(jnp.float32), n=n, axis=0)
    y = jnp.fft.irfft(U * H[None], n=n, axis=1)[:, L // 2:L // 2 + L]
    return y.astype(u.dtype)


def hyena_mixer(u, w_in, b_in, conv_w, conv_b, f_w_in, f_w_hid, f_b, f_freq, f_w_out, bias, w_out, b_out):
    L = u.shape[1]
    proj = short_conv3(u @ w_in + b_in, conv_w, conv_b)
    x0, x1, v = jnp.split(proj, 3, axis=-1)
    h = implicit_filter(L, f_w_in, f_w_hid, f_b, f_freq, f_w_out)
    v = v * x1
    y = x0 * (centred_long_conv(v, h) + v * bias)
    return y @ w_out + b_out


def mla_queries(cq, q_norm, wq_b, cos, sin):
    B, L, _ = cq.shape
    q = (rms_norm(cq, q_norm) @ wq_b).reshape(B, L, MLA_HEADS, QK_NOPE + QK_ROPE)
    q_nope, q_rope = q[..., :QK_NOPE], q[..., QK_NOPE:]
    if cos is not None:
        q_rope = apply_axial_rope(q_rope, cos[None, :, None], sin[None, :, None])
    return q_nope, q_rope


def mla_keys_values(ckv, k_rope, kv_norm, wkv_b, cos, sin):
    B, L, _ = ckv.shape
    kv = (rms_norm(ckv, kv_norm) @ wkv_b).reshape(B, L, MLA_HEADS, QK_NOPE + V_HEAD)
    k_nope, v = kv[..., :QK_NOPE], kv[..., QK_NOPE:]
    if cos is not None:
        k_rope = apply_axial_rope(k_rope, cos[None], sin[None])
    return k_nope, k_rope, v


def mla_attend(q_nope, q_rope, k_nope, k_rope, v):
    B, Lq = q_nope.shape[:2]
    nb = Lq // Q_BLOCK
    qn = q_nope.reshape(B, nb, Q_BLOCK, MLA_HEADS, QK_NOPE).transpose(1, 0, 2, 3, 4)
    qr = q_rope.reshape(B, nb, Q_BLOCK, MLA_HEADS, QK_ROPE).transpose(1, 0, 2, 3, 4)

    def block(args):
        qn_b, qr_b = args
        s = jnp.einsum('bqhd,bkhd->bhqk', qn_b, k_nope) + jnp.einsum('bqhr,bkr->bhqk', qr_b, k_rope)
        p = jax.nn.softmax(s.astype(jnp.float32) * ATTN_SCALE, axis=-1).astype(v.dtype)
        return jnp.einsum('bhqk,bkhd->bqhd', p, v)

    o = lax.map(block, (qn, qr))
    return o.transpose(1, 0, 2, 3, 4).reshape(B, Lq, MLA_HEADS * V_HEAD)


def pool_mixer(u, w_grp, scale):
    B, L, D = u.shape
    uf = u.astype(jnp.float32).reshape(B, L, POOL_GROUPS, POOL_CH)
    cs = jnp.concatenate([jnp.zeros((B, 1, POOL_GROUPS, POOL_CH), jnp.float32), jnp.cumsum(uf, axis=1)], axis=1)
    t = jnp.arange(L)
    outs = []
    for g, w in enumerate(POOL_WINDOWS):
        lo = jnp.clip(t - w // 2, 0, L)
        hi = jnp.clip(t + w // 2, 0, L)
        csg = cs[:, :, g]
        s = jnp.take(csg, hi, axis=1) - jnp.take(csg, lo, axis=1)
        mean = s / (hi - lo).astype(jnp.float32)[None, :, None]
        outs.append(mean - uf[:, :, g])
    d = jnp.stack(outs, axis=2).astype(u.dtype)
    y = jnp.einsum('blgc,gce->blge', d, w_grp).reshape(B, L, D)
    return y * scale


def setup_inputs(seed: int = 0) -> dict:
    key = jax.random.key(seed)
    ks = iter(jax.random.split(key, 40))

    def nrm(shape, s):
        return jax.random.normal(next(ks), shape, jnp.float32) * s

    D = D_MODEL
    inp = {}
    inp['x'] = nrm((BATCH, SEQ, D), 1.0)
    inp['c'] = nrm((BATCH, D), 1.0)
    inp['ctx'] = nrm((BATCH, CTX_LEN, D), 1.0)
    inp['c_ctx'] = nrm((D,), 1.0)
    inp['ada_w'] = nrm((DEPTH, D, 6 * D), 0.5 * D ** -0.5)
    inp['ada_b'] = nrm((DEPTH, 6 * D), 0.02)
    inp['ln_g'] = 1.0 + nrm((DEPTH, 2, D), 0.02)
    inp['ln_b'] = nrm((DEPTH, 2, D), 0.02)
    inp['ffn_w_gate'] = nrm((DEPTH, D, D_FF), D ** -0.5)
    inp['ffn_w_up'] = nrm((DEPTH, D, D_FF), D ** -0.5)
    inp['ffn_w_down'] = nrm((DEPTH, D_FF, D), BETA * D_FF ** -0.5)
    inp['hy_w_in'] = nrm((N_HYENA, D, 3 * D), D ** -0.5)
    inp['hy_b_in'] = nrm((N_HYENA, 3 * D), 0.02)
    inp['hy_conv_w'] = nrm((N_HYENA, 3, 3 * D), 3 ** -0.5)
    inp['hy_conv_b'] = nrm((N_HYENA, 3 * D), 0.02)
    inp['hy_f_w_in'] = nrm((N_HYENA, HY_EMB, HY_FILT), HY_EMB ** -0.5)
    inp['hy_f_w_hid'] = nrm((N_HYENA, 2, HY_FILT, HY_FILT), HY_FILT ** -0.5)
    inp['hy_f_b'] = nrm((N_HYENA, 3, HY_FILT), 0.02)
    inp['hy_f_freq'] = 1.0 + nrm((N_HYENA, 3, HY_FILT), 0.1)
    inp['hy_f_w_out'] = nrm((N_HYENA, HY_FILT, D), HY_FILT ** -0.5)
    inp['hy_bias'] = nrm((N_HYENA, D), 1.0)
    inp['hy_w_out'] = nrm((N_HYENA, D, D), BETA * D ** -0.5)
    inp['hy_b_out'] = nrm((N_HYENA, D), 0.02)
    inp['mla_w_in'] = nrm((N_MLA, D, Q_RANK + KV_RANK + QK_ROPE), D ** -0.5)
    inp['mla_q_norm'] = 1.0 + nrm((N_MLA, Q_RANK), 0.02)
    inp['mla_kv_norm'] = 1.0 + nrm((N_MLA, KV_RANK), 0.02)
    inp['mla_wq_b'] = nrm((N_MLA, Q_RANK, MLA_HEADS * (QK_NOPE + QK_ROPE)), Q_RANK ** -0.5)
    inp['mla_wkv_b'] = nrm((N_MLA, KV_RANK, MLA_HEADS * (QK_NOPE + V_HEAD)), KV_RANK ** -0.5)
    inp['mla_w_out'] = nrm((N_MLA, MLA_HEADS * V_HEAD, D), BETA * (MLA_HEADS * V_HEAD) ** -0.5)
    inp['pool_w'] = nrm((N_POOL, POOL_GROUPS, POOL_CH, POOL_CH), BETA * POOL_CH ** -0.5)
    inp['pool_scale'] = 1.0 + nrm((N_POOL, D), 0.1)
    return inp


def reference(x, c, ctx, c_ctx, ada_w, ada_b, ln_g, ln_b, ffn_w_gate, ffn_w_up, ffn_w_down,
              hy_w_in, hy_b_in, hy_conv_w, hy_conv_b, hy_f_w_in, hy_f_w_hid, hy_f_b, hy_f_freq,
              hy_f_w_out, hy_bias, hy_w_out, hy_b_out,
              mla_w_in, mla_q_norm, mla_kv_norm, mla_wq_b, mla_wkv_b, mla_w_out,
              pool_w, pool_scale):
    L = x.shape[1]
    cos, sin = axial_rope_tables(L)
    mla_layers = [i for i in range(DEPTH) if i % N_MIXERS == 1]
    last_ctx_read = mla_layers[-1] if mla_layers else -1
    silu_c = jax.nn.silu(c)
    silu_cc = jax.nn.silu(c_ctx)
    h, hc = x, ctx
    for i in range(DEPTH):
        kind, j = i % N_MIXERS, i // N_MIXERS
        ctx_update = i < last_ctx_read
        sh1, sc1, g1, sh2, sc2, g2 = [t[:, None, :] for t in jnp.split(silu_c @ ada_w[i] + ada_b[i], 6, axis=-1)]
        if kind == 1 or ctx_update:
            csh1, csc1, cg1, csh2, csc2, cg2 = jnp.split(silu_cc @ ada_w[i] + ada_b[i], 6, axis=-1)
            uc = modulate(hc, csh1, csc1)
        u = modulate(h, sh1, sc1)
        if kind == 0:
            hy = (hy_w_in[j], hy_b_in[j], hy_conv_w[j], hy_conv_b[j], hy_f_w_in[j], hy_f_w_hid[j],
                  hy_f_b[j], hy_f_freq[j], hy_f_w_out[j], hy_bias[j], hy_w_out[j], hy_b_out[j])
            y = hyena_mixer(u, *hy)
            if ctx_update:
                yc = hyena_mixer(uc, *hy)
        elif kind == 1:
            w_in = mla_w_in[j]
            if ctx_update:
                cq_c, ckv_c, kr_c = jnp.split(uc @ w_in, [Q_RANK, Q_RANK + KV_RANK], axis=-1)
            else:
                ckv_c, kr_c = jnp.split(uc @ w_in[:, Q_RANK:], [KV_RANK], axis=-1)
            kn_c, kr_c, v_c = mla_keys_values(ckv_c, kr_c, mla_kv_norm[j], mla_wkv_b[j], None, None)
            cq, ckv, kr = jnp.split(u @ w_in, [Q_RANK, Q_RANK + KV_RANK], axis=-1)
            qn, qr = mla_queries(cq, mla_q_norm[j], mla_wq_b[j], cos, sin)
            kn, kr, v = mla_keys_values(ckv, kr, mla_kv_norm[j], mla_wkv_b[j], cos, sin)
            o = mla_attend(qn, qr, jnp.concatenate([kn, kn_c], axis=1),
                           jnp.concatenate([kr, kr_c], axis=1), jnp.concatenate([v, v_c], axis=1))
            y = o @ mla_w_out[j]
            if ctx_update:
                qn_c, qr_c = mla_queries(cq_c, mla_q_norm[j], mla_wq_b[j], None, None)
                yc = mla_attend(qn_c, qr_c, kn_c, kr_c, v_c) @ mla_w_out[j]
        else:
            y = pool_mixer(u, pool_w[j], pool_scale[j])
            if ctx_update:
                yc = pool_mixer(uc, pool_w[j], pool_scale[j])
        ffn = (ffn_w_gate[i], ffn_w_up[i], ffn_w_down[i])
        h = post_norm(h, g1 * y, ln_g[i, 0], ln_b[i, 0])
        h = post_norm(h, g2 * swiglu(modulate(h, sh2, sc2), *ffn), ln_g[i, 1], ln_b[i, 1])
        if ctx_update:
            hc = post_norm(hc, cg1 * yc, ln_g[i, 0], ln_b[i, 0])
            hc = post_norm(hc, cg2 * swiglu(modulate(hc, csh2, csc2), *ffn), ln_g[i, 1], ln_b[i, 1])
    return h
```

```python
import contextlib
import math
import numpy as np
import ml_dtypes
import concourse.bass as bass
import concourse.mybir as mybir
from concourse.bass_utils import run_bass_kernel_spmd

F32, BF16 = mybir.dt.float32, mybir.dt.bfloat16
AF, ALU, AX = mybir.ActivationFunctionType, mybir.AluOpType, mybir.AxisListType
NPBF = ml_dtypes.bfloat16
NCORES = 8
D = 2048
DFF = 5632
KC = D // 128
FC = DFF // 128
DEPTH = 4
ALPHA = (2.0 * DEPTH) ** 0.25
LN_EPS = 1e-6
RMS_EPS = 1e-6
TB = 512


class Buf:
    __slots__ = ("w", "r", "dsem", "dkey", "dcnt")

    def __init__(self):
        self.w = {}
        self.r = {}
        self.dsem = None
        self.dkey = None
        self.dcnt = 0


class Tile:
    def __init__(self, t, nb=1):
        self.t = t
        self.b = [Buf() for _ in range(nb)]


def _merge(dst, src):
    for k, v in src.items():
        if dst.get(k, 0) < v:
            dst[k] = v


class Sched:
    def __init__(self, nc, stack):
        self.nc = nc
        self.stack = stack
        self.E = {"pe": nc.tensor, "act": nc.scalar, "dve": nc.vector, "pool": nc.gpsimd, "sp": nc.sync}
        self.semobj = {}
        self.cnt = {}
        for e in ("pe", "act", "dve", "pool"):
            self.semobj[e] = stack.enter_context(nc.semaphore("sem_" + e))
            self.cnt[e] = 0
        self.waited = {e: {} for e in self.E}
        self.alltok = {}
        self.nd = 0

    def sb(self, shape, dtype, nb=1, name=None):
        self.nd += 1
        t = self.stack.enter_context(self.nc.sbuf_tensor(name or ("sb%d" % self.nd), list(shape), dtype))
        return Tile(t, nb)

    def ps(self, shape, dtype=F32, name=None):
        self.nd += 1
        t = self.stack.enter_context(self.nc.psum_tensor(name or ("ps%d" % self.nd), list(shape), dtype))
        return Tile(t, 1)

    def _wait(self, eng, toks):
        wd = self.waited[eng]
        for k, v in toks.items():
            if k == eng and eng == "pe":
                continue
            if wd.get(k, 0) >= v:
                continue
            wd[k] = v
            self.E[eng].wait_ge(self.semobj[k], v)

    def _deps(self, reads, writes, waw):
        toks = {}
        for b in reads:
            _merge(toks, b.w)
        for b in writes:
            _merge(toks, b.r)
            if waw:
                _merge(toks, b.w)
        return toks

    def op(self, eng, fn, reads=(), writes=(), inc=True, waw=True):
        self._wait(eng, self._deps(reads, writes, waw))
        ins = fn(self.E[eng])
        n = self.cnt[eng] + 1
        if inc:
            self.cnt[eng] = n
            ins.then_inc(self.semobj[eng], 1)
            self.alltok[eng] = n
        for b in reads:
            if b.r.get(eng, 0) < n:
                b.r[eng] = n
        for b in writes:
            if b.w.get(eng, 0) < n:
                b.w[eng] = n

    def dma(self, q, out, in_, reads=(), writes=(), owner=None, waw=True):
        self._wait(q, self._deps(reads, writes, waw))
        if owner is None:
            owner = writes[0]
        if owner.dsem is None:
            self.nd += 1
            owner.dkey = "d%d" % self.nd
            owner.dsem = self.stack.enter_context(self.nc.semaphore("sem_" + owner.dkey))
            self.semobj[owner.dkey] = owner.dsem
        owner.dcnt += 16
        self.E[q].dma_start(out=out, in_=in_).then_inc(owner.dsem, 16)
        k, n = owner.dkey, owner.dcnt
        self.alltok[k] = n
        for b in reads:
            if b.r.get(k, 0) < n:
                b.r[k] = n
        for b in writes:
            if b.w.get(k, 0) < n:
                b.w[k] = n

    def finish(self):
        for e in ("sp", "act", "dve", "pool", "pe"):
            self._wait(e, dict(self.alltok))


class WStream:
    def __init__(self, S, nslots, slot_elems, dtype=BF16, q="sp"):
        self.S = S
        self.slots = [S.sb([128, slot_elems], dtype) for _ in range(nslots)]
        self.n = nslots
        self.plan = []
        self.issued = 0
        self.used = 0
        self.q = q

    def add(self, ap, elems):
        self.plan.append((ap, elems))

    def _issue(self):
        ap, elems = self.plan[self.issued]
        sl = self.slots[self.issued % self.n]
        self.S.dma(self.q, sl.t[:, :elems], ap, writes=sl.b)
        self.issued += 1

    def get(self):
        i = self.used
        while self.issued < min(len(self.plan), i + self.n):
            self._issue()
        self.used += 1
        return self.slots[i % self.n]


def new_nc():
    return bass.Bass("TRN2", target_bir_lowering=False)


def run_spmd(nc, in_maps):
    res = run_bass_kernel_spmd(nc, in_maps, core_ids=list(range(NCORES)))
    return res.results


def build_cast(n, C=4096):
    nc = new_nc()
    x = nc.dram_tensor("x", [128, n], F32, kind="ExternalInput").ap()
    y = nc.dram_tensor("y", [128, n], BF16, kind="ExternalOutput").ap()
    with contextlib.ExitStack() as st:
        S = Sched(nc, st)
        NB = 3
        xin = [S.sb([128, C], F32) for _ in range(NB)]
        yo = [S.sb([128, C], BF16) for _ in range(NB)]
        for i, c0 in enumerate(range(0, n, C)):
            w = min(C, n - c0)
            s = i % NB
            S.dma("sp", xin[s].t[:, :w], x[:, c0:c0 + w], writes=xin[s].b)
            if i % 2:
                S.op("act", lambda e: e.copy(out=yo[s].t[:, :w], in_=xin[s].t[:, :w]), reads=xin[s].b, writes=yo[s].b)
            else:
                S.op("dve", lambda e: e.tensor_copy(out=yo[s].t[:, :w], in_=xin[s].t[:, :w]), reads=xin[s].b, writes=yo[s].b)
            S.dma("pool", y[:, c0:c0 + w], yo[s].t[:, :w], reads=yo[s].b, owner=yo[s].b[0])
        S.finish()
    return nc


def device_cast(flat):
    n_tot = flat.size
    per = -(-n_tot // (NCORES * 128))
    per = -(-per // 64) * 64
    pad = np.zeros(NCORES * 128 * per, np.float32)
    pad[:n_tot] = flat
    pad = pad.reshape(NCORES, 128, per)
    nc = build_cast(per)
    res = run_spmd(nc, [{"x": pad[c]} for c in range(NCORES)])
    out = np.concatenate([np.asarray(r["y"]).reshape(-1) for r in res])
    return out[:n_tot]


def w_layout(W):
    K_, M_ = W.shape
    return np.ascontiguousarray(W.reshape(K_ // 128, 128, M_ // 128, 128).transpose(2, 1, 0, 3))


def vec_layout(v):
    return np.ascontiguousarray(v.reshape(-1, 128).T)


def act_layout(xT):
    T_, F_ = xT.shape
    return np.ascontiguousarray(xT.T.reshape(F_ // 128, 128, T_))


def act_unlayout(a):
    c, p, t = a.shape
    return np.ascontiguousarray(a.reshape(c * p, t).T)


MOD_MC = 4 * 96 // NCORES


def build_mod():
    nc = new_nc()
    w = nc.dram_tensor("w", [MOD_MC, 128, KC * 128], F32, kind="ExternalInput").ap()
    bvec = nc.dram_tensor("b", [128, MOD_MC], F32, kind="ExternalInput").ap()
    cT = nc.dram_tensor("cT", [128, KC * 4], F32, kind="ExternalInput").ap()
    y = nc.dram_tensor("y", [128, MOD_MC * 4], F32, kind="ExternalOutput").ap()
    with contextlib.ExitStack() as st:
        S = Sched(nc, st)
        ct = S.sb([128, KC * 4], F32)
        sT = S.sb([128, KC * 4], F32)
        bt = S.sb([128, MOD_MC], F32)
        yo = S.sb([128, MOD_MC * 4], F32)
        S.dma("sp", ct.t[:, :], cT, writes=ct.b)
        S.dma("sp", bt.t[:, :], bvec, writes=bt.b)
        S.op("act", lambda e: e.activation(out=sT.t[:, :], in_=ct.t[:, :], func=AF.Silu), reads=ct.b, writes=sT.b)
        ws = WStream(S, 3, KC * 128, F32)
        for m in range(MOD_MC):
            ws.add(w[m], KC * 128)
        pss = [S.ps([128, 512]) for _ in range(2)]
        for m in range(MOD_MC):
            sl = ws.get()
            p = pss[m % 2]
            for kc in range(KC):
                S.op("pe", lambda e: e.matmul(p.t[:, 0:4], lhsT=sl.t[:, kc * 128:(kc + 1) * 128], rhs=sT.t[:, kc * 4:kc * 4 + 4],
                                              start=(kc == 0), stop=(kc == KC - 1)),
                     reads=sl.b + sT.b, writes=p.b, inc=(kc == KC - 1))
            S.op("dve", lambda e: e.tensor_scalar(out=yo.t[:, m * 4:m * 4 + 4], in0=p.t[:, 0:4], scalar1=bt.t[:, m:m + 1], scalar2=None,
                                                  op0=ALU.add), reads=p.b + bt.b, writes=yo.b, waw=False)
        S.dma("sp", y, yo.t[:, :], reads=yo.b, owner=yo.b[0])
        S.finish()
    return nc


def device_mod(c, c_ctx, ada_w, ada_b):
    cv = np.zeros((4, D), np.float32)
    cv[0], cv[1], cv[2] = c[0], c[1], c_ctx
    cT = np.ascontiguousarray(cv.reshape(4, KC, 128).transpose(2, 1, 0)).reshape(128, KC * 4)
    Wl = np.concatenate([w_layout(ada_w[i]) for i in range(DEPTH)], axis=0)
    Wl = Wl.reshape(NCORES, MOD_MC, 128, KC * 128)
    bl = ada_b.reshape(NCORES, MOD_MC, 128).transpose(0, 2, 1)
    nc = build_mod()
    res = run_spmd(nc, [{"w": Wl[k], "b": np.ascontiguousarray(bl[k]), "cT": cT} for k in range(NCORES)])
    ys = np.stack([np.asarray(r["y"]).reshape(128, MOD_MC, 4) for r in res])
    mod = ys.transpose(0, 2, 1, 3).reshape(DEPTH, 6 * D, 4)[:, :, :3]
    return np.ascontiguousarray(mod)


class Ctx:
    def __init__(self, nc, st):
        self.nc = nc
        self.S = Sched(nc, st)
        S = self.S
        self.banks = [S.ps([128, 512]) for _ in range(6)]
        self.st1 = S.ps([128, 512])
        self.st2 = S.ps([128, 512])
        self.bi = 0
        self.ones = S.sb([128, 128], F32)
        S.op("dve", lambda e: e.memset(self.ones.t[:, :], 1.0), writes=self.ones.b)

    def bank(self):
        b = self.banks[self.bi % len(self.banks)]
        self.bi += 1
        return b


def layer_norm_fm(C, XA, XB, T, gcol, bcol, outs):
    S = C.S
    sq = C.sqs
    for kc in range(KC):
        S.op("pe", lambda e: e.matmul(C.st1.t[:, :T], lhsT=C.ones.t[:, :], rhs=XA(kc), start=(kc == 0), stop=(kc == KC - 1)),
             reads=C.ones.b + [XB[kc]], writes=C.st1.b, inc=(kc == KC - 1))
    for kc in range(KC):
        s = sq[kc % len(sq)]
        S.op("act", lambda e: e.activation(out=s.t[:, :T], in_=XA(kc), func=AF.Square), reads=[XB[kc]], writes=s.b)
        S.op("pe", lambda e: e.matmul(C.st2.t[:, :T], lhsT=C.ones.t[:, :], rhs=s.t[:, :T], start=(kc == 0), stop=(kc == KC - 1)),
             reads=C.ones.b + s.b, writes=C.st2.b, inc=True)
    mean, var, rstd, nmr = C.lnt
    S.op("dve", lambda e: e.tensor_scalar(out=mean.t[:, :T], in0=C.st1.t[:, :T], scalar1=1.0 / D, scalar2=None, op0=ALU.mult),
         reads=C.st1.b, writes=mean.b)
    S.op("dve", lambda e: e.tensor_tensor(out=var.t[:, :T], in0=mean.t[:, :T], in1=mean.t[:, :T], op=ALU.mult), reads=mean.b, writes=var.b)
    S.op("dve", lambda e: e.scalar_tensor_tensor(out=var.t[:, :T], in0=C.st2.t[:, :T], scalar=1.0 / D, in1=var.t[:, :T],
                                                 op0=ALU.mult, op1=ALU.subtract), reads=C.st2.b + var.b, writes=var.b)
    S.op("dve", lambda e: e.tensor_scalar(out=var.t[:, :T], in0=var.t[:, :T], scalar1=LN_EPS, scalar2=None, op0=ALU.add), reads=var.b, writes=var.b)
    S.op("act", lambda e: e.activation(out=var.t[:, :T], in_=var.t[:, :T], func=AF.Sqrt), reads=var.b, writes=var.b)
    S.op("dve", lambda e: e.reciprocal(out=rstd.t[:, :T], in_=var.t[:, :T]), reads=var.b, writes=rstd.b)
    S.op("dve", lambda e: e.scalar_tensor_tensor(out=nmr.t[:, :T], in0=mean.t[:, :T], scalar=-1.0, in1=rstd.t[:, :T],
                                                 op0=ALU.mult, op1=ALU.mult), reads=mean.b + rstd.b, writes=nmr.b)
    for kc in range(KC):
        t2 = C.lnw[kc % len(C.lnw)]
        S.op("dve", lambda e: e.scalar_tensor_tensor(out=t2.t[:, :T], in0=XA(kc), scalar=gcol(kc), in1=rstd.t[:, :T],
                                                     op0=ALU.mult, op1=ALU.mult), reads=[XB[kc]] + rstd.b, writes=t2.b)
        S.op("dve", lambda e: e.scalar_tensor_tensor(out=t2.t[:, :T], in0=nmr.t[:, :T], scalar=gcol(kc), in1=t2.t[:, :T],
                                                     op0=ALU.mult, op1=ALU.add), reads=nmr.b + t2.b, writes=t2.b)
        S.op("act", lambda e: e.activation(out=XA(kc), in_=t2.t[:, :T], func=AF.Identity, bias=bcol(kc), scale=1.0),
             reads=t2.b, writes=[XB[kc]])
        for fn in outs:
            fn(kc, t2)


def gemm_fm(C, ws, MC, KCn, rhs_fn, rhs_bufs_fn, T, epi):
    S = C.S
    for mc in range(MC):
        sl = ws.get()
        p = C.bank()
        for kc in range(KCn):
            S.op("pe", lambda e: e.matmul(p.t[:, :T], lhsT=sl.t[:, kc * 128:(kc + 1) * 128], rhs=rhs_fn(kc), start=(kc == 0), stop=(kc == KCn - 1)),
                 reads=sl.b + rhs_bufs_fn(kc), writes=p.b, inc=(kc == KCn - 1))
        epi(mc, p)


def build_tail(nblk, blk_set, nsets, front="gemm"):
    NT = nblk * TB
    T = TB
    HAL = 8 if front == "pool" else 0
    nc = new_nc()
    hT = nc.dram_tensor("hT", [KC, 128, NT + 2 * HAL], F32, kind="ExternalInput").ap()
    if front != "pool":
        yin = nc.dram_tensor("yin", [KC, 128, NT], BF16, kind="ExternalInput").ap()
        wout = nc.dram_tensor("wout", [KC, 128, KC * 128], BF16, kind="ExternalInput").ap()
    else:
        wout = nc.dram_tensor("wout", [KC, 128, 4 * 128], BF16, kind="ExternalInput").ap()
        rc = nc.dram_tensor("rc", [128, 4, NT], F32, kind="ExternalInput").ap()
        pmask = nc.dram_tensor("pmask", [128, 2 * nblk], F32, kind="ExternalInput").ap()
    if front == "hyena":
        x0i = nc.dram_tensor("x0", [KC, 128, NT], BF16, kind="ExternalInput").ap()
        vgi = nc.dram_tensor("vg", [KC, 128, NT], BF16, kind="ExternalInput").ap()
    wg = nc.dram_tensor("wg", [FC, 128, KC * 128], BF16, kind="ExternalInput").ap()
    wu = nc.dram_tensor("wu", [FC, 128, KC * 128], BF16, kind="ExternalInput").ap()
    wd = nc.dram_tensor("wd", [KC, 128, FC * 128], BF16, kind="ExternalInput").ap()
    modv = nc.dram_tensor("modv", [128, nsets * 6 * KC], F32, kind="ExternalInput").ap()
    lnv = nc.dram_tensor("lnv", [128, 7 * KC], F32, kind="ExternalInput").ap()
    oT = nc.dram_tensor("oT", [KC, 128, NT], F32, kind="ExternalOutput").ap()
    with contextlib.ExitStack() as st:
        C = Ctx(nc, st)
        S = C.S
        A = S.sb([128, KC, T + 2 * HAL], F32, nb=KC)
        B = S.sb([128, KC, T], F32, nb=KC)
        uT = S.sb([128, KC, T], BF16, nb=KC)
        actT = S.sb([128, FC, T], BF16, nb=FC)
        C.sqs = [S.sb([128, T], F32) for _ in range(2)]
        C.lnt = [S.sb([128, T], F32) for _ in range(4)]
        C.lnw = [S.sb([128, T], F32) for _ in range(2)]
        sg = [S.sb([128, T], F32) for _ in range(2)]
        tt = [S.sb([128, T], F32) for _ in range(2)]
        mv = S.sb([128, nsets * 6 * KC], F32)
        lv = S.sb([128, 7 * KC], F32)
        dv = S.sb([128, nsets * 6 * KC], F32)
        S.dma("sp", mv.t[:, :], modv, writes=mv.b)
        S.dma("sp", lv.t[:, :], lnv, writes=lv.b)
        if front == "pool":
            pm = S.sb([128, 2 * nblk], F32)
            S.dma("sp", pm.t[:, :], pmask, writes=pm.b)
            rct = S.sb([128, 4, T], F32)
            ue = [S.sb([128, T + 16], F32) for _ in range(2)]
            wk = [S.sb([128, T + 16], F32) for _ in range(2)]

        def AC(kc):
            return A.t[:, kc, HAL:HAL + T]

        def MV(s, i, kc=None):
            o = (s * 6 + i) * KC
            return mv.t[:, o:o + KC] if kc is None else mv.t[:, o + kc:o + kc + 1]

        def LV(i, kc=None):
            o = i * KC
            return lv.t[:, o:o + KC] if kc is None else lv.t[:, o + kc:o + kc + 1]

        def DV(s, i, kc=None):
            o = (s * 6 + i) * KC
            return dv.t[:, o:o + KC] if kc is None else dv.t[:, o + kc:o + kc + 1]

        rd = mv.b + lv.b
        for s in range(nsets):
            S.op("dve", lambda e: e.tensor_tensor(out=DV(s, 0), in0=MV(s, 2), in1=LV(5), op=ALU.mult), reads=rd, writes=dv.b)
            S.op("dve", lambda e: e.tensor_tensor(out=DV(s, 1), in0=DV(s, 0), in1=LV(4), op=ALU.mult), reads=rd + dv.b, writes=dv.b)
            S.op("dve", lambda e: e.tensor_scalar(out=DV(s, 2), in0=MV(s, 4), scalar1=1.0, scalar2=None, op0=ALU.add), reads=rd, writes=dv.b)
            S.op("dve", lambda e: e.tensor_tensor(out=DV(s, 3), in0=DV(s, 2), in1=LV(1), op=ALU.mult), reads=rd + dv.b, writes=dv.b)
            S.op("dve", lambda e: e.tensor_tensor(out=DV(s, 3), in0=DV(s, 3), in1=MV(s, 3), op=ALU.add), reads=rd + dv.b, writes=dv.b)
            S.op("dve", lambda e: e.tensor_scalar(out=DV(s, 4), in0=MV(s, 1), scalar1=1.0, scalar2=None, op0=ALU.add), reads=rd, writes=dv.b)
        cst = rd + dv.b

        KM = 4 if front == "pool" else KC
        ws = WStream(S, 3 if front == "pool" else 4, FC * 128)
        for blk in range(nblk):
            for mc in range(KC):
                ws.add(wout[mc], KM * 128)
            for fc in range(FC):
                ws.add(wg[fc], KC * 128)
                ws.add(wu[fc], KC * 128)
            for mc in range(KC):
                ws.add(wd[mc], FC * 128)

        for blk in range(nblk):
            s = blk_set[blk]
            t0 = blk * T
            S.dma("pool", A.t[:, :, :], hT[:, :, t0:t0 + T + 2 * HAL].rearrange("k p t -> p k t"), writes=A.b)
            yt = actT
            if front in ("gemm", "hyena"):
                S.dma("pool", yt.t[:, 0:KC, :], yin[:, :, t0:t0 + T].rearrange("k p t -> p k t"), writes=yt.b[0:KC])
            if front == "hyena":
                S.dma("pool", yt.t[:, KC:2 * KC, :], vgi[:, :, t0:t0 + T].rearrange("k p t -> p k t"), writes=yt.b[KC:2 * KC])
                S.dma("pool", uT.t[:, :, :], x0i[:, :, t0:t0 + T].rearrange("k p t -> p k t"), writes=uT.b)
                for kc in range(KC):
                    S.op("dve", lambda e: e.scalar_tensor_tensor(out=yt.t[:, KC + kc, :], in0=yt.t[:, KC + kc, :], scalar=LV(6, kc), in1=yt.t[:, kc, :],
                                                                 op0=ALU.mult, op1=ALU.add), reads=[yt.b[KC + kc], yt.b[kc]] + cst, writes=[yt.b[KC + kc]])
                    S.op("dve", lambda e: e.tensor_tensor(out=yt.t[:, kc, :], in0=yt.t[:, KC + kc, :], in1=uT.t[:, kc, :], op=ALU.mult),
                         reads=[yt.b[KC + kc], uT.b[kc]], writes=[yt.b[kc]])
            if front == "pool":
                S.dma("pool", rct.t[:, :, :], rc[:, :, t0:t0 + T], writes=rct.b)
                W_ = T + 16
                for kc in range(KC):
                    g = kc // 4
                    u_ = ue[kc % 2]
                    a_ = wk[0]
                    b_ = wk[1]
                    S.op("act", lambda e: e.activation(out=u_.t[:, :], in_=A.t[:, kc, :], func=AF.Identity, bias=MV(s, 0, kc), scale=DV(s, 4, kc)),
                         reads=[A.b[kc]] + cst, writes=u_.b)
                    S.op("dve", lambda e: e.tensor_scalar(out=u_.t[:, 0:8], in0=u_.t[:, 0:8], scalar1=pm.t[:, 2 * blk:2 * blk + 1], scalar2=None,
                                                          op0=ALU.mult), reads=u_.b + pm.b, writes=u_.b)
                    S.op("dve", lambda e: e.tensor_scalar(out=u_.t[:, T + 8:T + 16], in0=u_.t[:, T + 8:T + 16], scalar1=pm.t[:, 2 * blk + 1:2 * blk + 2],
                                                          scalar2=None, op0=ALU.mult), reads=u_.b + pm.b, writes=u_.b)
                    S.op("dve", lambda e: e.tensor_tensor(out=a_.t[:, 1:W_], in0=u_.t[:, 0:W_ - 1], in1=u_.t[:, 1:W_], op=ALU.add), reads=u_.b, writes=a_.b)
                    cur, oth, lo, hi = a_, b_, 1, W_
                    for lvl in range(g):
                        sh = 1 << lvl
                        nlo, nhi = lo + sh, hi - sh
                        S.op("dve", lambda e: e.tensor_tensor(out=oth.t[:, nlo:nhi], in0=cur.t[:, nlo - sh:nhi - sh], in1=cur.t[:, nlo + sh:nhi + sh],
                                                              op=ALU.add), reads=cur.b, writes=oth.b)
                        cur, oth, lo, hi = oth, cur, nlo, nhi
                    S.op("dve", lambda e: e.tensor_tensor(out=cur.t[:, 8:T + 8], in0=cur.t[:, 8:T + 8], in1=rct.t[:, g, :], op=ALU.mult),
                         reads=cur.b + rct.b, writes=cur.b)
                    S.op("dve", lambda e: e.tensor_tensor(out=yt.t[:, kc, :], in0=cur.t[:, 8:T + 8], in1=u_.t[:, 8:T + 8], op=ALU.subtract),
                         reads=cur.b + u_.b, writes=[yt.b[kc]])

            def epi1(mc, p):
                t = tt[mc % 2]
                S.op("act", lambda e: e.activation(out=t.t[:, :], in_=p.t[:, :T], func=AF.Identity, bias=DV(s, 1, mc), scale=DV(s, 0, mc)),
                     reads=p.b + cst, writes=t.b)
                S.op("dve", lambda e: e.scalar_tensor_tensor(out=B.t[:, mc, :], in0=AC(mc), scalar=ALPHA, in1=t.t[:, :],
                                                             op0=ALU.mult, op1=ALU.add), reads=[A.b[mc]] + t.b, writes=[B.b[mc]])
            if front == "pool":
                for mc in range(KC):
                    sl = ws.get()
                    p = C.bank()
                    for k4 in range(4):
                        kc = (mc // 4) * 4 + k4
                        S.op("pe", lambda e: e.matmul(p.t[:, :T], lhsT=sl.t[:, k4 * 128:(k4 + 1) * 128], rhs=yt.t[:, kc, :], start=(k4 == 0), stop=(k4 == 3)),
                             reads=sl.b + [yt.b[kc]], writes=p.b, inc=(k4 == 3))
                    epi1(mc, p)
            else:
                gemm_fm(C, ws, KC, KC, lambda kc: yt.t[:, kc, :], lambda kc: [yt.b[kc]], T, epi1)

            def mk_u(kc, t2):
                S.op("act", lambda e: e.activation(out=uT.t[:, kc, :], in_=t2.t[:, :T], func=AF.Identity, bias=DV(s, 3, kc), scale=DV(s, 2, kc)),
                     reads=t2.b + cst, writes=[uT.b[kc]])
            layer_norm_fm(C, lambda kc: B.t[:, kc, :], B.b, T, lambda kc: LV(0, kc), lambda kc: LV(1, kc), [mk_u])

            for fc in range(FC):
                pg, pu = C.bank(), C.bank()
                for p in (pg, pu):
                    sl = ws.get()
                    for kc in range(KC):
                        S.op("pe", lambda e: e.matmul(p.t[:, :T], lhsT=sl.t[:, kc * 128:(kc + 1) * 128], rhs=uT.t[:, kc, :],
                                                      start=(kc == 0), stop=(kc == KC - 1)),
                             reads=sl.b + [uT.b[kc]], writes=p.b, inc=(kc == KC - 1))
                g = sg[fc % 2]
                S.op("act", lambda e: e.activation(out=g.t[:, :], in_=pg.t[:, :T], func=AF.Silu), reads=pg.b, writes=g.b)
                S.op("dve", lambda e: e.tensor_tensor(out=actT.t[:, fc, :], in0=g.t[:, :], in1=pu.t[:, :T], op=ALU.mult),
                     reads=g.b + pu.b, writes=[actT.b[fc]])

            def epi2(mc, p):
                t = tt[mc % 2]
                S.op("act", lambda e: e.activation(out=t.t[:, :], in_=p.t[:, :T], func=AF.Identity, scale=MV(s, 5, mc)),
                     reads=p.b + cst, writes=t.b)
                S.op("dve", lambda e: e.scalar_tensor_tensor(out=AC(mc), in0=B.t[:, mc, :], scalar=ALPHA, in1=t.t[:, :],
                                                             op0=ALU.mult, op1=ALU.add), reads=[B.b[mc]] + t.b, writes=[A.b[mc]])
            gemm_fm(C, ws, KC, FC, lambda fc: actT.t[:, fc, :], lambda fc: [actT.b[fc]], T, epi2)

            layer_norm_fm(C, AC, A.b, T, lambda kc: LV(2, kc), lambda kc: LV(3, kc), [])
            S.dma("pool", oT[:, :, t0:t0 + T].rearrange("k p t -> p k t"), A.t[:, :, HAL:HAL + T], reads=A.b, owner=A.b[1])
        S.finish()
    return nc


def build_hyA(segs, next_cols, nout, nsets):
    nc = new_nc()
    hX = nc.dram_tensor("hX", [KC, 128, next_cols], F32, kind="ExternalInput").ap()
    win = nc.dram_tensor("win", [3 * KC, 128, KC * 128], BF16, kind="ExternalInput").ap()
    modv = nc.dram_tensor("modv", [128, nsets * 2 * KC], F32, kind="ExternalInput").ap()
    cv = nc.dram_tensor("cv", [128, 5 * 3 * KC], F32, kind="ExternalInput").ap()
    edge = nc.dram_tensor("edge", [128, 2 * len(segs)], F32, kind="ExternalInput").ap()
    x0o = nc.dram_tensor("x0o", [KC, 128, nout], BF16, kind="ExternalOutput").ap()
    vgo = nc.dram_tensor("vgo", [KC, 128, nout], BF16, kind="ExternalOutput").ap()
    with contextlib.ExitStack() as st:
        C = Ctx(nc, st)
        S = C.S
        hx = S.sb([128, KC, 512], F32, nb=KC)
        uT = S.sb([128, KC, 512], BF16, nb=KC)
        pb = [S.sb([128, 512], F32) for _ in range(3)]
        q = [S.sb([128, 512], F32) for _ in range(3)]
        xo = [S.sb([128, KC, 512], BF16, nb=KC) for _ in range(2)]
        mv = S.sb([128, nsets * 2 * KC], F32)
        cvt = S.sb([128, 5 * 3 * KC], F32)
        eg = S.sb([128, 2 * len(segs)], F32)
        S.dma("sp", mv.t[:, :], modv, writes=mv.b)
        S.dma("sp", cvt.t[:, :], cv, writes=cvt.b)
        S.dma("sp", eg.t[:, :], edge, writes=eg.b)
        for s in range(nsets):
            o = (s * 2 + 1) * KC
            S.op("dve", lambda e: e.tensor_scalar(out=mv.t[:, o:o + KC], in0=mv.t[:, o:o + KC], scalar1=1.0, scalar2=None, op0=ALU.add),
                 reads=mv.b, writes=mv.b)
        cst = mv.b + cvt.b + eg.b

        def CV(i, mc):
            return cvt.t[:, i * 3 * KC + mc:i * 3 * KC + mc + 1]

        ws = WStream(S, 4, KC * 128)
        for _ in segs:
            for c in range(KC):
                for third in range(3):
                    ws.add(win[third * KC + c], KC * 128)
        oc = 0
        for si, (c0, N, s) in enumerate(segs):
            n = N - 2
            S.dma("pool", hx.t[:, :, :N], hX[:, :, c0:c0 + N].rearrange("k p t -> p k t"), writes=hx.b)
            for kc in range(KC):
                S.op("act", lambda e: e.activation(out=uT.t[:, kc, :N], in_=hx.t[:, kc, :N], func=AF.Identity,
                                                   bias=mv.t[:, (s * 2) * KC + kc:(s * 2) * KC + kc + 1],
                                                   scale=mv.t[:, (s * 2 + 1) * KC + kc:(s * 2 + 1) * KC + kc + 1]),
                     reads=[hx.b[kc]] + cst, writes=[uT.b[kc]])
            for c in range(KC):
                for third in range(3):
                    mc = third * KC + c
                    sl = ws.get()
                    p = C.bank()
                    for kc in range(KC):
                        S.op("pe", lambda e: e.matmul(p.t[:, :N], lhsT=sl.t[:, kc * 128:(kc + 1) * 128], rhs=uT.t[:, kc, :N],
                                                      start=(kc == 0), stop=(kc == KC - 1)),
                             reads=sl.b + [uT.b[kc]], writes=p.b, inc=(kc == KC - 1))
                    b_ = pb[third]
                    q_ = q[third]
                    S.op("act", lambda e: e.activation(out=b_.t[:, :N], in_=p.t[:, :N], func=AF.Identity, bias=CV(0, mc), scale=1.0),
                         reads=p.b + cst, writes=b_.b)
                    S.op("dve", lambda e: e.tensor_scalar(out=b_.t[:, 0:1], in0=b_.t[:, 0:1], scalar1=eg.t[:, 2 * si:2 * si + 1], scalar2=None,
                                                          op0=ALU.mult), reads=b_.b + cst, writes=b_.b)
                    S.op("dve", lambda e: e.tensor_scalar(out=b_.t[:, N - 1:N], in0=b_.t[:, N - 1:N], scalar1=eg.t[:, 2 * si + 1:2 * si + 2],
                                                          scalar2=None, op0=ALU.mult), reads=b_.b + cst, writes=b_.b)
                    S.op("dve", lambda e: e.tensor_scalar(out=q_.t[:, :n], in0=b_.t[:, 1:N - 1], scalar1=CV(2, mc), scalar2=CV(4, mc),
                                                          op0=ALU.mult, op1=ALU.add), reads=b_.b + cst, writes=q_.b)
                    S.op("dve", lambda e: e.scalar_tensor_tensor(out=q_.t[:, :n], in0=b_.t[:, 0:N - 2], scalar=CV(1, mc), in1=q_.t[:, :n],
                                                                 op0=ALU.mult, op1=ALU.add), reads=b_.b + q_.b + cst, writes=q_.b)
                    if third == 0:
                        S.op("dve", lambda e: e.scalar_tensor_tensor(out=xo[0].t[:, c, :n], in0=b_.t[:, 2:N], scalar=CV(3, mc), in1=q_.t[:, :n],
                                                                     op0=ALU.mult, op1=ALU.add), reads=b_.b + q_.b + cst, writes=[xo[0].b[c]])
                    else:
                        S.op("dve", lambda e: e.scalar_tensor_tensor(out=q_.t[:, :n], in0=b_.t[:, 2:N], scalar=CV(3, mc), in1=q_.t[:, :n],
                                                                     op0=ALU.mult, op1=ALU.add), reads=b_.b + q_.b + cst, writes=q_.b)
                S.op("dve", lambda e: e.tensor_tensor(out=xo[1].t[:, c, :n], in0=q[1].t[:, :n], in1=q[2].t[:, :n], op=ALU.mult),
                     reads=q[1].b + q[2].b, writes=[xo[1].b[c]])
            S.dma("pool", x0o[:, :, oc:oc + n].rearrange("k p t -> p k t"), xo[0].t[:, :, :n], reads=xo[0].b, owner=xo[0].b[0])
            S.dma("pool", vgo[:, :, oc:oc + n].rearrange("k p t -> p k t"), xo[1].t[:, :, :n], reads=xo[1].b, owner=xo[1].b[0])
            oc += n
        S.finish()
    return nc


HY_FILT = 64
HY_EMB = 33
TWO_PI = 2.0 * math.pi


def build_hyB(L, nb):
    nblk = L // 128
    NBK = nb * nblk
    P = 128
    LP = L + 2 * P
    PB = min(512, L)
    npb = L // PB
    nc = new_nc()
    Vh = nc.dram_tensor("Vh", [2, 128, 128 * NBK], BF16, kind="ExternalInput").ap()
    zr = nc.dram_tensor("zr", [HY_EMB, L], F32, kind="ExternalInput").ap()
    distr_h = nc.dram_tensor("distr", [1, L], F32, kind="ExternalInput")
    fwin = nc.dram_tensor("fwin", [HY_EMB, HY_FILT], F32, kind="ExternalInput").ap()
    fwhid = nc.dram_tensor("fwhid", [HY_FILT, 2 * HY_FILT], F32, kind="ExternalInput").ap()
    fvec = nc.dram_tensor("fvec", [HY_FILT, 6], F32, kind="ExternalInput").ap()
    fwout = nc.dram_tensor("fwout", [HY_FILT, 256], F32, kind="ExternalInput").ap()
    ndelta = nc.dram_tensor("ndelta", [128, 2], F32, kind="ExternalInput").ap()
    ident = nc.dram_tensor("ident", [128, 128], F32, kind="ExternalInput").ap()
    Yh = nc.dram_tensor("Yh", [256, 128, NBK], BF16, kind="ExternalOutput").ap()
    Hs_h = nc.dram_tensor("Hs", [256, LP], BF16)
    Hs = Hs_h.ap()
    with contextlib.ExitStack() as st:
        S = Sched(nc, st)
        HsB = Buf()
        ones = S.sb([128, 128], F32)
        S.op("dve", lambda e: e.memset(ones.t[:, :], 1.0), writes=ones.b)
        idt = S.sb([128, 128], F32)
        S.dma("sp", idt.t[:, :], ident, writes=idt.b)
        w1 = S.sb([HY_EMB, HY_FILT], F32)
        wh = S.sb([HY_FILT, 2 * HY_FILT], F32)
        fv = S.sb([HY_FILT, 8], F32)
        wo = S.sb([HY_FILT, 256], F32)
        nd = S.sb([128, 2], F32)
        S.dma("sp", w1.t[:, :], fwin, writes=w1.b)
        S.dma("sp", wh.t[:, :], fwhid, writes=wh.b)
        S.dma("sp", fv.t[:, 0:6], fvec, writes=fv.b)
        S.dma("sp", wo.t[:, :], fwout, writes=wo.b)
        S.dma("sp", nd.t[:, :], ndelta, writes=nd.b)
        cst = w1.b + wh.b + fv.b + wo.b + nd.b
        zt = [S.sb([HY_EMB, PB], F32) for _ in range(2)]
        db = [S.sb([128, PB], F32) for _ in range(2)]
        gt = [S.sb([HY_FILT, PB], F32) for _ in range(3)]
        dec = [S.sb([128, PB], F32) for _ in range(2)]
        ff = [S.sb([128, PB], F32) for _ in range(2)]
        fb = [S.sb([128, PB], BF16) for _ in range(4)]
        sm = S.sb([128, 2, npb], F32)
        ki = S.sb([HY_FILT, PB], mybir.dt.int32)
        kf = S.sb([HY_FILT, PB], F32)
        S.op("dve", lambda e: e.tensor_scalar(out=fv.t[:, 3:6], in0=fv.t[:, 3:6], scalar1=1.0 / TWO_PI, scalar2=None, op0=ALU.mult), reads=fv.b, writes=fv.b)
        zero = S.sb([128, P], BF16)
        S.op("dve", lambda e: e.memset(zero.t[:, :], 0.0), writes=zero.b)
        pmlp = [S.ps([128, 512]) for _ in range(3)]
        pconv = [S.ps([128, nb, nblk]) for _ in range(2)]
        pmisc = S.ps([128, 512])
        for cc in range(2):
            S.dma("sp", Hs[cc * 128:(cc + 1) * 128, 0:P], zero.t[:, :], reads=zero.b, writes=[HsB], owner=zero.b[0], waw=False)
            S.dma("sp", Hs[cc * 128:(cc + 1) * 128, P + L:LP], zero.t[:, :], reads=zero.b, writes=[HsB], owner=zero.b[0], waw=False)
        pi_ = 0
        for blk in range(npb):
            z_ = zt[blk % 2]
            d_ = db[blk % 2]
            S.dma("sp", z_.t[:, :], zr[:, blk * PB:(blk + 1) * PB], writes=z_.b)
            S.dma("sp", d_.t[:, :], bass.AP(distr_h, blk * PB, [[0, 128], [1, PB]]), writes=d_.b)
            src, srcK, srcb = z_, HY_EMB, z_.b
            for layer in range(3):
                p = pmlp[pi_ % 3]
                pi_ += 1
                lhsT = w1.t[:, :] if layer == 0 else wh.t[:, (layer - 1) * HY_FILT:layer * HY_FILT]
                S.op("pe", lambda e: e.matmul(p.t[0:HY_FILT, :PB], lhsT=lhsT, rhs=src.t[0:srcK, :PB], start=True, stop=True),
                     reads=cst + srcb, writes=p.b)
                g_ = gt[layer]
                S.op("dve", lambda e: e.tensor_scalar(out=g_.t[:, :], in0=p.t[0:HY_FILT, :PB], scalar1=fv.t[:, layer:layer + 1],
                                                      scalar2=fv.t[:, 3 + layer:4 + layer], op0=ALU.add, op1=ALU.mult), reads=p.b + cst, writes=g_.b)
                S.op("dve", lambda e: e.tensor_copy(out=ki.t[:, :], in_=g_.t[:, :]), reads=g_.b, writes=ki.b)
                S.op("dve", lambda e: e.tensor_copy(out=kf.t[:, :], in_=ki.t[:, :]), reads=ki.b, writes=kf.b)
                S.op("dve", lambda e: e.tensor_tensor(out=g_.t[:, :], in0=g_.t[:, :], in1=kf.t[:, :], op=ALU.subtract), reads=g_.b + kf.b, writes=g_.b)
                S.op("act", lambda e: e.activation(out=g_.t[:, :], in_=g_.t[:, :], func=AF.Sin, scale=6.283185), reads=g_.b, writes=g_.b)
                src, srcK, srcb = g_, HY_FILT, g_.b
            for cc in range(2):
                p = pmlp[pi_ % 3]
                pi_ += 1
                S.op("pe", lambda e: e.matmul(p.t[:, :PB], lhsT=wo.t[:, cc * 128:(cc + 1) * 128], rhs=gt[2].t[:, :PB], start=True, stop=True),
                     reads=cst + gt[2].b, writes=p.b)
                dc = dec[cc]
                S.op("act", lambda e: e.activation(out=dc.t[:, :], in_=d_.t[:, :], func=AF.Exp, scale=nd.t[:, cc:cc + 1]),
                     reads=d_.b + cst, writes=dc.b)
                f_ = ff[cc]
                S.op("dve", lambda e: e.tensor_tensor(out=f_.t[:, :], in0=p.t[:, :PB], in1=dc.t[:, :], op=ALU.mult), reads=p.b + dc.b, writes=f_.b)
                S.op("dve", lambda e: e.tensor_reduce(out=sm.t[:, cc, blk:blk + 1], in_=f_.t[:, :], axis=AX.X, op=ALU.add,
                                                      apply_absolute_value=True), reads=f_.b, writes=sm.b)
                o_ = fb[(2 * blk + cc) % 4]
                S.op("act", lambda e: e.copy(out=o_.t[:, :], in_=f_.t[:, :]), reads=f_.b, writes=o_.b)
                S.dma("sp", Hs[cc * 128:(cc + 1) * 128, P + blk * PB:P + (blk + 1) * PB], o_.t[:, :], reads=o_.b, writes=[HsB], owner=o_.b[0], waw=False)
        tot = S.sb([128, 2], F32)
        rnB = S.sb([128, 256], F32)
        dg = S.sb([128, 128], F32)
        for cc in range(2):
            S.op("dve", lambda e: e.tensor_reduce(out=tot.t[:, cc:cc + 1], in_=sm.t[:, cc, :], axis=AX.X, op=ALU.add), reads=sm.b, writes=tot.b)
        S.op("dve", lambda e: e.reciprocal(out=tot.t[:, :], in_=tot.t[:, :]), reads=tot.b, writes=tot.b)
        for cc in range(2):
            S.op("dve", lambda e: e.tensor_scalar(out=dg.t[:, :], in0=idt.t[:, :], scalar1=tot.t[:, cc:cc + 1], scalar2=None, op0=ALU.mult),
                 reads=idt.b + tot.b, writes=dg.b)
            S.op("pe", lambda e: e.matmul(pmisc.t[:, 0:128], lhsT=ones.t[:, :], rhs=dg.t[:, :], start=True, stop=True), reads=ones.b + dg.b, writes=pmisc.b)
            S.op("dve", lambda e: e.tensor_copy(out=rnB.t[:, cc * 128:(cc + 1) * 128], in_=pmisc.t[:, 0:128]), reads=pmisc.b, writes=rnB.b)
        Vall = S.sb([128, 128, nb, nblk], BF16)
        Rt = [S.sb([128, L + P], BF16) for _ in range(2)]
        GRP = 16
        ys = [S.sb([128, GRP, NBK], BF16) for _ in range(2)]
        Mx = nblk // 2
        lags = [0] + [m for m in range(-Mx, Mx + 1) if m != 0]
        ci = 0
        for cc in range(2):
            S.dma("pool", Vall.t[:, :, :, :], Vh[cc].rearrange("p (c b a) -> p c b a", c=128, b=nb), writes=Vall.b)
            for ch in range(128):
                gch = cc * 128 + ch
                R = Rt[ci % 2]
                S.dma("sp", R.t[:, :], bass.AP(Hs_h, gch * LP, [[1, 128], [1, L + P]]), reads=[HsB], writes=R.b)
                p = pconv[ci % 2]
                for li, m in enumerate(lags):
                    base = P + L // 2 - 128 * (m + 1)
                    lo, hi = max(0, m), min(nblk, nblk + m)
                    S.op("pe", lambda e: e.matmul(p.t[:, :, lo:hi], lhsT=R.t[:, base:base + 128], rhs=Vall.t[:, ch, :, lo - m:hi - m],
                                                  start=(li == 0), stop=(li == len(lags) - 1)),
                         reads=R.b + Vall.b, writes=p.b, inc=(li == len(lags) - 1))
                y_ = ys[(ci // GRP) % 2]
                k = ci % GRP
                S.op("act", lambda e: e.activation(out=y_.t[:, k, :], in_=p.t[:, :, :].rearrange("p b a -> p (b a)"), func=AF.Identity,
                                                   scale=rnB.t[:, gch:gch + 1]), reads=p.b + rnB.b, writes=y_.b, waw=False)
                if k == GRP - 1:
                    g0 = gch - (GRP - 1)
                    S.dma("pool", Yh[g0:g0 + GRP].rearrange("c p n -> p c n"), y_.t[:, :, :], reads=y_.b, owner=y_.b[0])
                ci += 1
        S.finish()
    return nc


HY_BANDS = (HY_EMB - 1) // 2
HY_MIN_DECAY = math.log(1e-2) / 1.5
HY_MAX_DECAY = math.log(1e-2) / 0.3
_PROG = {}


def prog(key, builder):
    if key not in _PROG:
        _PROG[key] = builder()
    return _PROG[key]


def cast_many(arrs):
    names = list(arrs)
    flat = np.concatenate([np.asarray(arrs[n], np.float32).reshape(-1) for n in names])
    out = device_cast(flat)
    res, o = {}, 0
    for n in names:
        sz = arrs[n].size
        res[n] = out[o:o + sz].reshape(arrs[n].shape)
        o += sz
    return res


def filt_tables(L):
    f32 = np.float32
    pos = np.arange(L, dtype=f32)
    t = pos / f32(L - 1)
    bands = np.linspace(1e-4, HY_BANDS - 1, HY_BANDS, dtype=f32)
    ang = (f32(2.0 * math.pi / L) * pos[:, None]) * bands[None, :]
    z = np.concatenate([t[:, None], np.cos(ang), -np.sin(ang)], axis=-1).astype(f32)
    dist = (np.abs(pos - f32(L // 2)) / f32(L // 2)).astype(f32)
    zr = np.ascontiguousarray(z[::-1].T)
    distr = np.ascontiguousarray(dist[::-1][None, :])
    return zr, distr


def modcols(mod, i, col):
    return np.concatenate([vec_layout(mod[i, k * D:(k + 1) * D, col]) for k in range(6)], axis=1)


def hyena_layer(i, j, H, Hc, SEQ, CTX, mod, inp, Wb, ctx_update):
    NTOT = 2 * SEQ
    NTc = NTOT // NCORES
    nseg = -(-NTc // 510)
    segs, col = [], 0
    for k in range(nseg):
        n = min(510, NTc - 510 * k)
        segs.append((510 * k, n + 2, 0))
    ncols = NTc + 2
    if ctx_update:
        for b in range(2):
            segs.append((ncols, CTX + 2, 1))
            ncols += CTX + 2
    nout = NTc + (2 * CTX if ctx_update else 0)
    nsets = 2 if ctx_update else 1
    ncA = prog(("hyA", tuple(segs), ncols, nout, nsets), lambda: build_hyA(segs, ncols, nout, nsets))
    cvv = np.concatenate([vec_layout(inp["hy_b_in"][j])] + [vec_layout(inp["hy_conv_w"][j][k]) for k in range(3)]
                         + [vec_layout(inp["hy_conv_b"][j])], axis=1)
    maps = []
    for c in range(NCORES):
        b, qd = c // 4, c % 4
        g0 = c * NTc
        hX = np.zeros((KC, 128, ncols), np.float32)
        hX[:, :, 1:1 + NTc] = H[:, :, g0:g0 + NTc]
        edge = np.ones((128, 2 * len(segs)), np.float32)
        if qd > 0:
            hX[:, :, 0] = H[:, :, g0 - 1]
        else:
            edge[:, 0] = 0.0
        if qd < 3:
            hX[:, :, 1 + NTc] = H[:, :, g0 + NTc]
        else:
            edge[:, 2 * (nseg - 1) + 1] = 0.0
        mv = [modcols(mod, i, b)[:, 0:2 * KC]]
        if ctx_update:
            o = NTc + 2
            for bb in range(2):
                hX[:, :, o + 1:o + 1 + CTX] = Hc[:, :, bb * CTX:(bb + 1) * CTX]
                edge[:, 2 * (nseg + bb):2 * (nseg + bb) + 2] = 0.0
                o += CTX + 2
            mv.append(modcols(mod, i, 2)[:, 0:2 * KC])
        maps.append({"hX": hX, "win": Wb["hy_win%d" % j].reshape(3 * KC, 128, KC * 128), "modv": np.concatenate(mv, axis=1),
                     "cv": cvv, "edge": edge})
    res = run_spmd(ncA, maps)
    X0 = np.concatenate([np.asarray(r["x0o"])[:, :, :NTc] for r in res], axis=2)
    VG = np.concatenate([np.asarray(r["vgo"])[:, :, :NTc] for r in res], axis=2)
    if ctx_update:
        X0c = np.asarray(res[0]["x0o"])[:, :, NTc:]
        VGc = np.asarray(res[0]["vgo"])[:, :, NTc:]
    def conv(VGx, L):
        nblk = L // 128
        NBK = 2 * nblk
        ncB = prog(("hyB", L), lambda: build_hyB(L, 2))
        zr, distr = filt_tables(L)
        deltas = np.abs(np.linspace(HY_MIN_DECAY, HY_MAX_DECAY, D, dtype=np.float32))
        fwhid = np.concatenate([inp["hy_f_w_hid"][j][0], inp["hy_f_w_hid"][j][1]], axis=1)
        fvec = np.concatenate([inp["hy_f_b"][j].T, inp["hy_f_freq"][j].T], axis=1)
        v2 = VGx.reshape(D, 2, nblk, 128)
        maps = []
        for c in range(NCORES):
            vh = v2[256 * c:256 * c + 256].reshape(2, 128, 2, nblk, 128).transpose(0, 4, 1, 2, 3)
            maps.append({"Vh": np.ascontiguousarray(vh).reshape(2, 128, 128 * NBK), "zr": zr, "distr": distr,
                         "fwin": np.ascontiguousarray(inp["hy_f_w_in"][j]), "fwhid": np.ascontiguousarray(fwhid),
                         "fvec": np.ascontiguousarray(fvec), "fwout": np.ascontiguousarray(inp["hy_f_w_out"][j][:, 256 * c:256 * c + 256]),
                         "ndelta": np.ascontiguousarray(-deltas[256 * c:256 * c + 256].reshape(2, 128).T),
                         "ident": np.eye(128, dtype=np.float32)})
        res = run_spmd(ncB, maps)
        Y = np.concatenate([np.asarray(r["Yh"]) for r in res], axis=0)
        Y = Y.reshape(D, 128, 2, nblk)[:, ::-1].transpose(0, 2, 3, 1)
        return np.ascontiguousarray(Y).reshape(KC, 128, 2 * L)
    YC = conv(VG, SEQ)
    if ctx_update:
        YCc = conv(VGc, CTX)
    nb_lat = NTc // TB
    nblk = nb_lat + (1 if ctx_update else 0)
    blk_set = [0] * nb_lat + ([1] if ctx_update else [])
    ncT = prog(("tail", nblk, tuple(blk_set), nsets, "hyena"), lambda: build_tail(nblk, blk_set, nsets, "hyena"))
    lnv = np.concatenate([vec_layout(v) for v in (inp["ln_g"][i, 0], inp["ln_b"][i, 0], inp["ln_g"][i, 1], inp["ln_b"][i, 1],
                                                  inp["hy_b_out"][j], np.ones(D, np.float32), inp["hy_bias"][j])], axis=1)
    maps = []
    for c in range(NCORES):
        b = c // 4
        g0 = c * NTc
        cat = (lambda a, ac: np.concatenate([a[:, :, g0:g0 + NTc], ac], axis=2)) if ctx_update else (lambda a, ac: np.ascontiguousarray(a[:, :, g0:g0 + NTc]))
        mv = [modcols(mod, i, b)] + ([modcols(mod, i, 2)] if ctx_update else [])
        maps.append({"hT": cat(H, Hc), "yin": cat(YC, YCc if ctx_update else None), "x0": cat(X0, X0c if ctx_update else None),
                     "vg": cat(VG, VGc if ctx_update else None), "wout": Wb["hy_wout%d" % j].reshape(KC, 128, KC * 128),
                     "wg": Wb["wg%d" % i].reshape(FC, 128, KC * 128), "wu": Wb["wu%d" % i].reshape(FC, 128, KC * 128),
                     "wd": Wb["wd%d" % i].reshape(KC, 128, FC * 128), "modv": np.concatenate(mv, axis=1), "lnv": lnv})
    res = run_spmd(ncT, maps)
    Hn = np.concatenate([np.asarray(r["oT"])[:, :, :NTc] for r in res], axis=2)
    Hcn = np.asarray(res[0]["oT"])[:, :, NTc:] if ctx_update else Hc
    return Hn, Hcn, dict(X0=X0, VG=VG, YC=YC)


NH = 16
QR = 512
ATTN_SCALE = (128 + 64) ** -0.5


def build_mlaA(nblk, blk_set, nsets):
    NT = nblk * TB
    T = TB
    nq = sum(1 for s in blk_set if s == 0) * TB
    nc = new_nc()
    hT = nc.dram_tensor("hT", [KC, 128, NT], F32, kind="ExternalInput").ap()
    w9 = nc.dram_tensor("w9", [9, 128, KC * 128], BF16, kind="ExternalInput").ap()
    wq = nc.dram_tensor("wq", [NH, 2, 128, 4 * 128], BF16, kind="ExternalInput").ap()
    modv = nc.dram_tensor("modv", [128, nsets * 2 * KC], F32, kind="ExternalInput").ap()
    nrm = nc.dram_tensor("nrm", [128, 8], F32, kind="ExternalInput").ap()
    rope = nc.dram_tensor("rope", [64, 2, NT], F32, kind="ExternalInput").ap()
    qn_o = nc.dram_tensor("qn", [NH, 128, nq], BF16, kind="ExternalOutput").ap()
    qr_o = nc.dram_tensor("qr", [NH, 64, nq], BF16, kind="ExternalOutput").ap()
    ckv_o = nc.dram_tensor("ckvn", [4, 128, NT], BF16, kind="ExternalOutput").ap()
    kr_o = nc.dram_tensor("kr", [64, NT], BF16, kind="ExternalOutput").ap()
    with contextlib.ExitStack() as st:
        C = Ctx(nc, st)
        S = C.S
        A = S.sb([128, KC, T], F32, nb=KC)
        uT = S.sb([128, KC, T], BF16, nb=KC)
        cf = S.sb([128, 4, T], F32, nb=4)
        cqn = S.sb([128, 4, T], BF16)
        ckn = S.sb([128, 4, T], BF16)
        qns = S.sb([128, NH, T], BF16)
        qrs = S.sb([64, NH, T], BF16)
        krs = S.sb([64, T], BF16)
        rp = S.sb([64, 2, T], F32)
        sq = [S.sb([128, T], F32) for _ in range(2)]
        rs = S.sb([128, T], F32)
        ta = [S.sb([64, T], F32) for _ in range(2)]
        mv = S.sb([128, nsets * 2 * KC], F32)
        nm = S.sb([128, 8], F32)
        S.dma("sp", mv.t[:, :], modv, writes=mv.b)
        S.dma("sp", nm.t[:, :], nrm, writes=nm.b)
        for s in range(nsets):
            o = (s * 2 + 1) * KC
            S.op("dve", lambda e: e.tensor_scalar(out=mv.t[:, o:o + KC], in0=mv.t[:, o:o + KC], scalar1=1.0, scalar2=None, op0=ALU.add),
                 reads=mv.b, writes=mv.b)
        cst = mv.b + nm.b
        ws = WStream(S, 4, KC * 128)
        for blk in range(nblk):
            for m in range(9):
                ws.add(w9[m], KC * 128)
            if blk_set[blk] == 0:
                for h in range(NH):
                    ws.add(wq[h, 0], 512)
                    ws.add(wq[h, 1], 512)
        qo = 0
        for blk in range(nblk):
            s = blk_set[blk]
            t0 = blk * T
            S.dma("pool", A.t[:, :, :], hT[:, :, t0:t0 + T].rearrange("k p t -> p k t"), writes=A.b)
            S.dma("pool", rp.t[:, :, :], rope[:, :, t0:t0 + T], writes=rp.b)
            for kc in range(KC):
                S.op("act", lambda e: e.activation(out=uT.t[:, kc, :], in_=A.t[:, kc, :], func=AF.Identity,
                                                   bias=mv.t[:, (s * 2) * KC + kc:(s * 2) * KC + kc + 1],
                                                   scale=mv.t[:, (s * 2 + 1) * KC + kc:(s * 2 + 1) * KC + kc + 1]),
                     reads=[A.b[kc]] + cst, writes=[uT.b[kc]])

            def rope_combine(pa, pb_, out_ap, out_bufs):
                a_, b_ = ta
                S.op("dve", lambda e: e.tensor_tensor(out=a_.t[:, :], in0=pa.t[0:64, :T], in1=rp.t[:, 0, :], op=ALU.mult), reads=pa.b + rp.b, writes=a_.b)
                S.op("dve", lambda e: e.tensor_tensor(out=b_.t[:, :], in0=pb_.t[0:64, :T], in1=rp.t[:, 1, :], op=ALU.mult), reads=pb_.b + rp.b, writes=b_.b)
                S.op("dve", lambda e: e.tensor_tensor(out=out_ap, in0=a_.t[:, :], in1=b_.t[:, :], op=ALU.add), reads=a_.b + b_.b, writes=out_bufs, waw=False)

            for grp, dst in ((0, cqn), (1, ckn)):
                for m4 in range(4):
                    sl = ws.get()
                    p = C.bank()
                    for kc in range(KC):
                        S.op("pe", lambda e: e.matmul(p.t[:, :T], lhsT=sl.t[:, kc * 128:(kc + 1) * 128], rhs=uT.t[:, kc, :], start=(kc == 0), stop=(kc == KC - 1)),
                             reads=sl.b + [uT.b[kc]], writes=p.b, inc=(kc == KC - 1))
                    S.op("act", lambda e: e.copy(out=cf.t[:, m4, :], in_=p.t[:, :T]), reads=p.b, writes=[cf.b[m4]])
                    q_ = sq[m4 % 2]
                    S.op("act", lambda e: e.activation(out=q_.t[:, :], in_=p.t[:, :T], func=AF.Square), reads=p.b, writes=q_.b)
                    S.op("pe", lambda e: e.matmul(C.st1.t[:, :T], lhsT=C.ones.t[:, :], rhs=q_.t[:, :], start=(m4 == 0), stop=(m4 == 3)),
                         reads=C.ones.b + q_.b, writes=C.st1.b)
                S.op("dve", lambda e: e.tensor_scalar(out=rs.t[:, :], in0=C.st1.t[:, :T], scalar1=1.0 / QR, scalar2=RMS_EPS, op0=ALU.mult, op1=ALU.add),
                     reads=C.st1.b, writes=rs.b)
                S.op("act", lambda e: e.activation(out=rs.t[:, :], in_=rs.t[:, :], func=AF.Sqrt), reads=rs.b, writes=rs.b)
                S.op("dve", lambda e: e.reciprocal(out=rs.t[:, :], in_=rs.t[:, :]), reads=rs.b, writes=rs.b)
                for m4 in range(4):
                    S.op("dve", lambda e: e.scalar_tensor_tensor(out=dst.t[:, m4, :], in0=cf.t[:, m4, :], scalar=nm.t[:, grp * 4 + m4:grp * 4 + m4 + 1],
                                                                 in1=rs.t[:, :], op0=ALU.mult, op1=ALU.mult), reads=[cf.b[m4]] + rs.b + cst, writes=dst.b, waw=False)
            S.dma("pool", ckv_o[:, :, t0:t0 + T].rearrange("k p t -> p k t"), ckn.t[:, :, :], reads=ckn.b, owner=ckn.b[0])
            sl = ws.get()
            pa, pb_ = C.bank(), C.bank()
            for half, p in ((0, pa), (1, pb_)):
                for kc in range(KC):
                    S.op("pe", lambda e: e.matmul(p.t[0:64, :T], lhsT=sl.t[:, kc * 128 + half * 64:kc * 128 + half * 64 + 64], rhs=uT.t[:, kc, :],
                                                  start=(kc == 0), stop=(kc == KC - 1)), reads=sl.b + [uT.b[kc]], writes=p.b, inc=(kc == KC - 1))
            rope_combine(pa, pb_, krs.t[:, :], krs.b)
            S.dma("pool", kr_o[:, t0:t0 + T], krs.t[:, :], reads=krs.b, owner=krs.b[0])
            if s != 0:
                continue
            for h in range(NH):
                sl = ws.get()
                p = C.bank()
                for k4 in range(4):
                    S.op("pe", lambda e: e.matmul(p.t[:, :T], lhsT=sl.t[:, k4 * 128:(k4 + 1) * 128], rhs=cqn.t[:, k4, :], start=(k4 == 0), stop=(k4 == 3)),
                         reads=sl.b + cqn.b, writes=p.b, inc=(k4 == 3))
                S.op("act", lambda e: e.copy(out=qns.t[:, h, :], in_=p.t[:, :T]), reads=p.b, writes=qns.b, waw=False)
                sl = ws.get()
                pa, pb_ = C.bank(), C.bank()
                for half, p in ((0, pa), (1, pb_)):
                    for k4 in range(4):
                        S.op("pe", lambda e: e.matmul(p.t[0:64, :T], lhsT=sl.t[:, k4 * 128 + half * 64:k4 * 128 + half * 64 + 64], rhs=cqn.t[:, k4, :],
                                                      start=(k4 == 0), stop=(k4 == 3)), reads=sl.b + cqn.b, writes=p.b, inc=(k4 == 3))
                rope_combine(pa, pb_, qrs.t[:, h, :], qrs.b)
            S.dma("pool", qn_o[:, :, qo:qo + T].rearrange("h p t -> p h t"), qns.t[:, :, :], reads=qns.b, owner=qns.b[0])
            S.dma("pool", qr_o[:, :, qo:qo + T].rearrange("h p t -> p h t"), qrs.t[:, :, :], reads=qrs.b, owner=qrs.b[0])
            qo += T
        S.finish()
    return nc


def build_mlaB(NQ, NK):
    NKT = NK // 128
    QG = NQ // 512
    KB = -(-NK // 512)
    nc = new_nc()
    qn = nc.dram_tensor("qn", [NH, 128, NQ], BF16, kind="ExternalInput").ap()
    qr = nc.dram_tensor("qr", [NH, 64, NQ], BF16, kind="ExternalInput").ap()
    ckvn = nc.dram_tensor("ckvn", [4, 128, NK], BF16, kind="ExternalInput").ap()
    kr = nc.dram_tensor("kr", [64, NK], BF16, kind="ExternalInput").ap()
    wkv = nc.dram_tensor("wkv", [NH, 2, 128, 512], BF16, kind="ExternalInput").ap()
    oT = nc.dram_tensor("oT", [NH, 128, NQ], BF16, kind="ExternalOutput").ap()
    with contextlib.ExitStack() as st:
        S = Sched(nc, st)
        krT = S.sb([64, NK], BF16)
        KhT = S.sb([128, NK], BF16)
        Vh = S.sb([128, NKT, 128], BF16)
        Qn = [S.sb([128, NQ], BF16) for _ in range(2)]
        Qr = [S.sb([64, NQ], BF16) for _ in range(2)]
        PT = [S.sb([128, 512], BF16) for _ in range(3)]
        cs = [S.sb([128, 4, 512], BF16) for _ in range(2)]
        wkt = [S.sb([128, 2, 512], BF16) for _ in range(2)]
        og = [S.sb([128, 512], BF16) for _ in range(2)]
        rec = [S.sb([128, 512], F32) for _ in range(2)]
        onesb = S.sb([128, 128], BF16)
        S.op("dve", lambda e: e.memset(onesb.t[:, :], 1.0), writes=onesb.b)
        pS = [S.ps([128, 512]) for _ in range(3)]
        pO = [S.ps([128, 512]) for _ in range(2)]
        pD = [S.ps([128, 512]) for _ in range(2)]
        pM = S.ps([128, 512])
        S.dma("sp", krT.t[:, :], kr, writes=krT.b)
        si = 0
        ci = 0
        for h in range(NH):
            w_ = wkt[h % 2]
            S.dma("sp", w_.t[:, :, :], wkv[h].rearrange("a p n -> p a n"), writes=w_.b)
            Qn_, Qr_ = Qn[h % 2], Qr[h % 2]
            S.dma("sp", Qn_.t[:, :], qn[h], writes=Qn_.b)
            S.dma("sp", Qr_.t[:, :], qr[h], writes=Qr_.b)
            for kb in range(KB):
                k0 = kb * 512
                n = min(512, NK - k0)
                c_ = cs[ci % 2]
                ci += 1
                S.dma("sp", c_.t[:, :, :n], ckvn[:, :, k0:k0 + n].rearrange("k p t -> p k t"), writes=c_.b)
                for k4 in range(4):
                    S.op("pe", lambda e: e.matmul(pM.t[:, :n], lhsT=w_.t[:, 0, k4 * 128:(k4 + 1) * 128], rhs=c_.t[:, k4, :n], start=(k4 == 0), stop=(k4 == 3)),
                         reads=w_.b + c_.b, writes=pM.b, inc=(k4 == 3))
                S.op("act", lambda e: e.copy(out=KhT.t[:, k0:k0 + n], in_=pM.t[:, :n]), reads=pM.b, writes=KhT.b, waw=False)
                for kt in range(n // 128):
                    for k4 in range(4):
                        S.op("pe", lambda e: e.matmul(pM.t[:, kt * 128:(kt + 1) * 128], lhsT=c_.t[:, k4, kt * 128:(kt + 1) * 128],
                                                      rhs=w_.t[:, 1, k4 * 128:(k4 + 1) * 128], start=(k4 == 0), stop=(k4 == 3)),
                             reads=w_.b + c_.b, writes=pM.b, inc=(k4 == 3 and kt == n // 128 - 1))
                S.op("dve", lambda e: e.tensor_copy(out=Vh.t[:, kb * 4:kb * 4 + n // 128, :], in_=pM.t[:, :n].rearrange("p (a d) -> p a d", d=128)),
                     reads=pM.b, writes=Vh.b, waw=False)
            for qg in range(QG):
                q0 = qg * 512
                po, pd = pO[qg % 2], pD[qg % 2]

                def smm(kt):
                    p = pS[(si + kt) % 3]
                    S.op("pe", lambda e: e.matmul(p.t[:, :], lhsT=KhT.t[:, kt * 128:(kt + 1) * 128], rhs=Qn_.t[:, q0:q0 + 512], start=True, stop=False),
                         reads=KhT.b + Qn_.b, writes=p.b, inc=False)
                    S.op("pe", lambda e: e.matmul(p.t[:, :], lhsT=krT.t[:, kt * 128:(kt + 1) * 128], rhs=Qr_.t[:, q0:q0 + 512], start=False, stop=True),
                         reads=krT.b + Qr_.b, writes=p.b)
                smm(0)
                for kt in range(NKT):
                    if kt + 1 < NKT:
                        smm(kt + 1)
                    p = pS[(si + kt) % 3]
                    pt = PT[(si + kt) % 3]
                    S.op("act", lambda e: e.activation(out=pt.t[:, :], in_=p.t[:, :], func=AF.Exp, scale=ATTN_SCALE), reads=p.b, writes=pt.b)
                    S.op("pe", lambda e: e.matmul(po.t[:, :], lhsT=Vh.t[:, kt, :], rhs=pt.t[:, :], start=(kt == 0), stop=(kt == NKT - 1)),
                         reads=Vh.b + pt.b, writes=po.b, inc=False)
                    S.op("pe", lambda e: e.matmul(pd.t[:, :], lhsT=onesb.t[:, :], rhs=pt.t[:, :], start=(kt == 0), stop=(kt == NKT - 1)),
                         reads=onesb.b + pt.b, writes=pd.b)
                si += NKT
                r_ = rec[qg % 2]
                o_ = og[qg % 2]
                S.op("dve", lambda e: e.reciprocal(out=r_.t[:, :], in_=pd.t[:, :]), reads=pd.b, writes=r_.b)
                S.op("dve", lambda e: e.tensor_tensor(out=o_.t[:, :], in0=po.t[:, :], in1=r_.t[:, :], op=ALU.mult), reads=po.b + r_.b, writes=o_.b)
                S.dma("pool", oT[h, :, q0:q0 + 512], o_.t[:, :], reads=o_.b, owner=o_.b[0])
        S.finish()
    return nc


GRID_W = 64
ROPE_PAIRS = 16
ROPE_THETA = 10000.0


def rope_tables(L):
    f32 = np.float32
    rows = L // GRID_W
    row = np.repeat(np.arange(rows, dtype=f32), GRID_W)
    colp = np.tile(np.arange(GRID_W, dtype=f32), rows)
    inv = (f32(ROPE_THETA) ** (-np.arange(ROPE_PAIRS, dtype=f32) / f32(ROPE_PAIRS))).astype(f32)
    ang = np.stack([row[:, None] * inv, colp[:, None] * inv], axis=1)
    ang = np.broadcast_to(ang[:, :, None, :], (L, 2, 2, ROPE_PAIRS)).reshape(L, 64)
    cos, sin = np.cos(ang).astype(f32), np.sin(ang).astype(f32)
    sign = np.tile(np.concatenate([-np.ones(16, f32), np.ones(16, f32)]), 2)
    perm = np.concatenate([np.arange(16, 32), np.arange(0, 16), np.arange(48, 64), np.arange(32, 48)])
    return np.ascontiguousarray(cos.T), np.ascontiguousarray((sin * sign[None, :]).T), perm


def mla_weights(inp):
    _, _, perm = rope_tables(GRID_W)
    w_in = inp["mla_w_in"][0]
    w8 = w_layout(w_in[:, :1024])
    wkr = np.concatenate([w_in[:, 1024:1088], w_in[:, 1024:1088][:, perm]], axis=1)
    w9 = np.concatenate([w8, w_layout(wkr)], axis=0)
    wqb = inp["mla_wq_b"][0].reshape(QR, NH, 192)
    wq = np.zeros((NH, 2, 128, 4, 128), np.float32)
    for h in range(NH):
        wq[h, 0] = w_layout(wqb[:, h, :128])[0]
        r = wqb[:, h, 128:]
        wq[h, 1] = w_layout(np.concatenate([r, r[:, perm]], axis=1))[0]
    wkvb = inp["mla_wkv_b"][0].reshape(QR, NH, 256)
    wkv = np.zeros((NH, 2, 128, 4, 128), np.float32)
    for h in range(NH):
        wkv[h, 0] = w_layout(wkvb[:, h, :128])[0]
        wkv[h, 1] = w_layout(wkvb[:, h, 128:])[0]
    return {"mla_w9": w9, "mla_wq": wq, "mla_wkv": wkv, "mla_wout": w_layout(inp["mla_w_out"][0])}


def tail_launch(i, front, H, extra, SEQ, mod, inp, Wb, lnv_extra, ctx=None):
    NTOT = 2 * SEQ
    NTc = NTOT // NCORES
    nblk = NTc // TB
    blk_set = [0] * nblk
    ncT = prog(("tail", nblk, tuple(blk_set), 1, front), lambda: build_tail(nblk, blk_set, 1, front))
    lnv = np.concatenate([vec_layout(v) for v in (inp["ln_g"][i, 0], inp["ln_b"][i, 0], inp["ln_g"][i, 1], inp["ln_b"][i, 1]) + tuple(lnv_extra)], axis=1)
    maps = []
    for c in range(NCORES):
        m = {"wg": Wb["wg%d" % i].reshape(FC, 128, KC * 128), "wu": Wb["wu%d" % i].reshape(FC, 128, KC * 128),
             "wd": Wb["wd%d" % i].reshape(KC, 128, FC * 128), "modv": modcols(mod, i, c // 4), "lnv": lnv}
        m.update(extra(c))
        maps.append(m)
    res = run_spmd(ncT, maps)
    return np.concatenate([np.asarray(r["oT"]) for r in res], axis=2)


def mla_layer(i, H, Hc, SEQ, CTX, mod, inp, Wb):
    NTOT = 2 * SEQ
    NTc = NTOT // NCORES
    nb_lat = NTc // TB
    nblk = nb_lat + (2 * CTX) // TB
    blk_set = [0] * nb_lat + [1] * ((2 * CTX) // TB)
    ncA = prog(("mlaA", nblk, tuple(blk_set)), lambda: build_mlaA(nblk, blk_set, 2))
    cosT, sinT, _ = rope_tables(SEQ)
    nrm = np.concatenate([vec_layout(inp["mla_q_norm"][0]), vec_layout(inp["mla_kv_norm"][0])], axis=1)
    maps = []
    for c in range(NCORES):
        b, qd = c // 4, c % 4
        g0 = c * NTc
        t0 = qd * NTc
        rope = np.zeros((64, 2, nblk * TB), np.float32)
        rope[:, 0, :NTc] = cosT[:, t0:t0 + NTc]
        rope[:, 1, :NTc] = sinT[:, t0:t0 + NTc]
        rope[:, 0, NTc:] = 1.0
        maps.append({"hT": np.concatenate([H[:, :, g0:g0 + NTc], Hc], axis=2), "w9": Wb["mla_w9"].reshape(9, 128, KC * 128),
                     "wq": Wb["mla_wq"].reshape(NH, 2, 128, 512),
                     "modv": np.concatenate([modcols(mod, i, b)[:, :2 * KC], modcols(mod, i, 2)[:, :2 * KC]], axis=1), "nrm": nrm, "rope": rope})
    res = run_spmd(ncA, maps)
    NK = SEQ + CTX
    ckv_b, kr_b = [], []
    for b in range(2):
        ck = [np.asarray(res[4 * b + q]["ckvn"])[:, :, :NTc] for q in range(4)] + [np.asarray(res[0]["ckvn"])[:, :, NTc + b * CTX:NTc + (b + 1) * CTX]]
        kk = [np.asarray(res[4 * b + q]["kr"])[:, :NTc] for q in range(4)] + [np.asarray(res[0]["kr"])[:, NTc + b * CTX:NTc + (b + 1) * CTX]]
        ckv_b.append(np.ascontiguousarray(np.concatenate(ck, axis=2)))
        kr_b.append(np.ascontiguousarray(np.concatenate(kk, axis=1)))
    ncB = prog(("mlaB", NTc, NK), lambda: build_mlaB(NTc, NK))
    maps = [{"qn": np.asarray(res[c]["qn"]), "qr": np.asarray(res[c]["qr"]), "ckvn": ckv_b[c // 4], "kr": kr_b[c // 4],
             "wkv": Wb["mla_wkv"].reshape(NH, 2, 128, 512)} for c in range(NCORES)]
    resB = run_spmd(ncB, maps)
    O = np.concatenate([np.asarray(r["oT"]) for r in resB], axis=2)
    ext = (np.zeros(D, np.float32), np.ones(D, np.float32), np.zeros(D, np.float32))
    Hn = tail_launch(i, "gemm", H, lambda c: {"hT": np.ascontiguousarray(H[:, :, c * NTc:(c + 1) * NTc]),
                                              "yin": np.ascontiguousarray(O[:, :, c * NTc:(c + 1) * NTc]),
                                              "wout": Wb["mla_wout"].reshape(KC, 128, KC * 128)}, SEQ, mod, inp, Wb, ext)
    return Hn, dict(O=O, kr=kr_b)


POOL_WINDOWS = (2, 4, 8, 16)


def pool_layer(i, H, SEQ, mod, inp, Wb):
    NTOT = 2 * SEQ
    NTc = NTOT // NCORES
    nblk = NTc // TB
    t = np.arange(SEQ)
    rcs = np.stack([1.0 / (np.clip(t + w // 2, 0, SEQ) - np.clip(t - w // 2, 0, SEQ)).astype(np.float32) for w in POOL_WINDOWS]).astype(np.float32)
    ext = (np.zeros(D, np.float32), inp["pool_scale"][0], np.zeros(D, np.float32))

    def extra(c):
        qd = c % 4
        g0 = c * NTc
        hX = np.zeros((KC, 128, NTc + 16), np.float32)
        hX[:, :, 8:8 + NTc] = H[:, :, g0:g0 + NTc]
        pm = np.ones((128, 2 * nblk), np.float32)
        if qd > 0:
            hX[:, :, 0:8] = H[:, :, g0 - 8:g0]
        else:
            pm[:, 0] = 0.0
        if qd < 3:
            hX[:, :, 8 + NTc:] = H[:, :, g0 + NTc:g0 + NTc + 8]
        else:
            pm[:, 2 * nblk - 1] = 0.0
        rc = np.ascontiguousarray(np.broadcast_to(rcs[None, :, qd * NTc:(qd + 1) * NTc], (128, 4, NTc)))
        return {"hT": hX, "wout": Wb["pool_w"].reshape(KC, 128, 512), "rc": rc, "pmask": pm}
    return tail_launch(i, "pool", H, extra, SEQ, mod, inp, Wb, ext)


def prep_weights(inp):
    W = {}
    for i in range(DEPTH):
        W["wg%d" % i] = w_layout(inp["ffn_w_gate"][i])
        W["wu%d" % i] = w_layout(inp["ffn_w_up"][i])
        W["wd%d" % i] = w_layout(inp["ffn_w_down"][i])
    for j in range(2):
        W["hy_win%d" % j] = w_layout(inp["hy_w_in"][j])
        W["hy_wout%d" % j] = w_layout(inp["hy_w_out"][j])
    W.update(mla_weights(inp))
    W["pool_w"] = np.concatenate([w_layout(inp["pool_w"][0][g]) for g in range(4)], axis=0)
    return cast_many(W)


def kernel(**inp):
    inp = {k: np.asarray(v) for k, v in inp.items()}
    x = inp["x"]
    B_, SEQ, _ = x.shape
    CTX = inp["ctx"].shape[1]
    mod = device_mod(inp["c"], inp["c_ctx"], inp["ada_w"], inp["ada_b"])
    Wb = prep_weights(inp)
    H = act_layout(x.reshape(B_ * SEQ, D))
    Hc = act_layout(inp["ctx"].reshape(B_ * CTX, D))
    H, Hc, _ = hyena_layer(0, 0, H, Hc, SEQ, CTX, mod, inp, Wb, True)
    H, _ = mla_layer(1, H, Hc, SEQ, CTX, mod, inp, Wb)
    H = pool_layer(2, H, SEQ, mod, inp, Wb)
    H, _, _ = hyena_layer(3, 1, H, Hc, SEQ, CTX, mod, inp, Wb, False)
    return act_unlayout(H).reshape(B_, SEQ, D).astype(np.float32)
```
